# Optimizing a Trainium2 kernel written in Bass

```python
import jax, jax.numpy as jnp
from jax import lax
import numpy as np

D_MODEL = 1024
BATCH = 16
SEQ = 2048
DEPTH = 4

N_MIXERS = 4
EXPAND = 2
EXP_WIDTH = EXPAND * D_MODEL
CHUNK = 128
GM_HEADS = 8
GM_HEAD_DIM = EXP_WIDTH // GM_HEADS
CONV_WIDTH = 31
SHORT_CONV_WIDTH = 3
POOL_WINDOWS = (2, 4, 8, 16)
POOL_GROUP = EXP_WIDTH // len(POOL_WINDOWS)
EPS = 1e-6
N_GM = (DEPTH + 3) // N_MIXERS
N_CV = (DEPTH + 2) // N_MIXERS
N_SC = (DEPTH + 1) // N_MIXERS
N_PL = DEPTH // N_MIXERS

kernel_name = "hybrid_interleaved_conv_pool_gmlp_encoder"


def rmsnorm(x, g):
    xf = x.astype(jnp.float32)
    y = xf * lax.rsqrt(jnp.mean(xf * xf, axis=-1, keepdims=True) + EPS)
    return y.astype(x.dtype) * g


def layernorm(x, g, b):
    xf = x.astype(jnp.float32)
    mu = jnp.mean(xf, axis=-1, keepdims=True)
    xc = xf - mu
    y = xc * lax.rsqrt(jnp.mean(xc * xc, axis=-1, keepdims=True) + EPS)
    return y.astype(x.dtype) * g + b


def depthwise_conv(x, w, pad):
    return lax.conv_general_dilated(
        x, w[:, None, :], window_strides=(1,), padding=[(pad, pad)],
        dimension_numbers=("NWC", "WIO", "NWC"), feature_group_count=x.shape[-1])


def chunked_gmlp(h, w_in, ln_g, ln_b, sp_w, sp_b, w_out):
    bsz, seq, _ = h.shape
    u, v, z = jnp.split(h @ w_in, 3, axis=-1)
    u = jax.nn.gelu(u)
    v = layernorm(jax.nn.gelu(v), ln_g, ln_b)
    v = v.reshape(bsz, seq // CHUNK, CHUNK, GM_HEADS, GM_HEAD_DIM)
    v = jnp.einsum("hpq,bnqhc->bnphc", sp_w, v) + sp_b.T[None, None, :, :, None]
    v = v.reshape(bsz, seq, EXP_WIDTH)
    return (u * v * jax.nn.silu(z)) @ w_out


def conformer_conv(h, w_in, dw_w, dw_b, ln_g, ln_b, w_out):
    a, g, z = jnp.split(h @ w_in, 3, axis=-1)
    y = a * jax.nn.sigmoid(g)
    y = depthwise_conv(y, dw_w, CONV_WIDTH // 2) + dw_b
    y = jax.nn.silu(layernorm(y, ln_g, ln_b))
    return (y * jax.nn.silu(z)) @ w_out


def short_gated_conv(h, w_in, conv_w, w_out):
    bg, cg, v, z = jnp.split(h @ w_in, 4, axis=-1)
    y = bg * depthwise_conv(cg * v, conv_w, SHORT_CONV_WIDTH // 2)
    return (y * jax.nn.silu(z)) @ w_out


def multiscale_pool(h, w_in, pool_w, pool_b, scale, w_out):
    p, z = jnp.split(h @ w_in, 2, axis=-1)
    bsz, seq, _ = p.shape
    pf = p.astype(jnp.float32)
    cs = jnp.concatenate([jnp.zeros((bsz, 1, EXP_WIDTH), jnp.float32),
                          jnp.cumsum(pf, axis=1)], axis=1)
    t = jnp.arange(seq)
    outs = []
    for gi, w in enumerate(POOL_WINDOWS):
        sl = slice(gi * POOL_GROUP, (gi + 1) * POOL_GROUP)
        lo = jnp.clip(t - w // 2, 0, seq - 1)
        hi = jnp.clip(t + w - 1 - w // 2, 0, seq - 1)
        csg = cs[..., sl]
        win_sum = jnp.take(csg, hi + 1, axis=1) - jnp.take(csg, lo, axis=1)
        mean = win_sum / (hi - lo + 1).astype(jnp.float32)[None, :, None]
        d = (mean - pf[..., sl]).astype(p.dtype)
        outs.append(d @ pool_w[gi] + pool_b[gi])
    y = jnp.concatenate(outs, axis=-1) * scale
    return (y * jax.nn.silu(z)) @ w_out


def setup_inputs(seed: int = 0) -> dict:
    key = jax.random.key(seed)
    ks = iter(jax.random.split(key, 40))
    nrm = lambda shape, s: jax.random.normal(next(ks), shape, jnp.float32) * s
    D, E = D_MODEL, EXP_WIDTH
    return {
        "x": nrm((BATCH, SEQ, D), 1.0),
        "c": nrm((BATCH, D), 1.0),
        "norm_g": 1.0 + nrm((DEPTH, D), 0.02),
        "ada_w": nrm((DEPTH, D, 3 * D), 0.5 * D ** -0.5),
        "ada_b": nrm((DEPTH, 3 * D), 0.02),
        "gm_w_in": nrm((N_GM, D, 3 * E), D ** -0.5),
        "gm_ln_g": 1.0 + nrm((N_GM, E), 0.02),
        "gm_ln_b": nrm((N_GM, E), 0.02),
        "gm_sp_w": nrm((N_GM, GM_HEADS, CHUNK, CHUNK), CHUNK ** -0.5),
        "gm_sp_b": 1.0 + nrm((N_GM, GM_HEADS, CHUNK), 0.02),
        "gm_w_out": nrm((N_GM, E, D), E ** -0.5),
        "cv_w_in": nrm((N_CV, D, 3 * E), D ** -0.5),
        "cv_dw_w": nrm((N_CV, CONV_WIDTH, E), CONV_WIDTH ** -0.5),
        "cv_dw_b": nrm((N_CV, E), 0.02),
        "cv_ln_g": 1.0 + nrm((N_CV, E), 0.02),
        "cv_ln_b": nrm((N_CV, E), 0.02),
        "cv_w_out": nrm((N_CV, E, D), E ** -0.5),
        "sc_w_in": nrm((N_SC, D, 4 * E), D ** -0.5),
        "sc_conv_w": nrm((N_SC, SHORT_CONV_WIDTH, E), SHORT_CONV_WIDTH ** -0.5),
        "sc_w_out": nrm((N_SC, E, D), E ** -0.5),
        "pl_w_in": nrm((N_PL, D, 2 * E), D ** -0.5),
        "pl_w": nrm((N_PL, len(POOL_WINDOWS), POOL_GROUP, POOL_GROUP), POOL_GROUP ** -0.5),
        "pl_b": nrm((N_PL, len(POOL_WINDOWS), POOL_GROUP), 0.02),
        "pl_scale": 1.0 + nrm((N_PL, E), 0.1),
        "pl_w_out": nrm((N_PL, E, D), E ** -0.5),
        "final_g": 1.0 + nrm((D,), 0.02),
    }


def reference(x, c, norm_g, ada_w, ada_b,
              gm_w_in, gm_ln_g, gm_ln_b, gm_sp_w, gm_sp_b, gm_w_out,
              cv_w_in, cv_dw_w, cv_dw_b, cv_ln_g, cv_ln_b, cv_w_out,
              sc_w_in, sc_conv_w, sc_w_out,
              pl_w_in, pl_w, pl_b, pl_scale, pl_w_out,
              final_g):
    c_act = jax.nn.silu(c)
    for i in range(DEPTH):
        kind, j = i % N_MIXERS, i // N_MIXERS
        mod = c_act @ ada_w[i] + ada_b[i]
        shift, scale, gate = jnp.split(mod, 3, axis=-1)
        h = rmsnorm(x, norm_g[i]) * (1.0 + scale[:, None, :]) + shift[:, None, :]
        if kind == 0:
            y = chunked_gmlp(h, gm_w_in[j], gm_ln_g[j], gm_ln_b[j],
                             gm_sp_w[j], gm_sp_b[j], gm_w_out[j])
        elif kind == 1:
            y = conformer_conv(h, cv_w_in[j], cv_dw_w[j], cv_dw_b[j],
                               cv_ln_g[j], cv_ln_b[j], cv_w_out[j])
        elif kind == 2:
            y = short_gated_conv(h, sc_w_in[j], sc_conv_w[j], sc_w_out[j])
        else:
            y = multiscale_pool(h, pl_w_in[j], pl_w[j], pl_b[j], pl_scale[j], pl_w_out[j])
        x = x + gate[:, None, :] * y
    return rmsnorm(x, final_g)
```

```python
import bisect
import numpy as np
import concourse.bass as bass
import concourse.mybir as mybir
from concourse.bass_utils import run_bass_kernel_spmd
from contextlib import ExitStack

F32 = mybir.dt.float32
BF16 = mybir.dt.bfloat16
AF = mybir.ActivationFunctionType
ALU = mybir.AluOpType

D = 1024
E = 2048
S = 2048
KD = 8
KE = 16
NT = 4
TW = 512
NSUB = 16
DEPTH = 4
EPS = 1e-6
POOL_W = (2, 4, 8, 16)
SLOT = 2048
NBUF = 3
N_CORES = 8

_off = {}
_n = 0
for _name, _w in [("normg", DEPTH * KD), ("gm_lng", KE), ("gm_lnb", KE),
                  ("cv_dwb", KE), ("cv_lng", KE), ("cv_lnb", KE),
                  ("sc_cw", KE * 3), ("pl_b", KE), ("pl_s", KE), ("invc", 4 * 16)]:
    _off[_name] = (_n, _w)
    _n += _w
NSMALL = _n


class Prog:
    def __init__(self, nc, es):
        self.nc = nc
        self.es = es
        self.eng = dict(pe=nc.tensor, act=nc.scalar, dve=nc.vector, pool=nc.gpsimd, sp=nc.sync)
        self.sems = {}
        self.count = {}
        self.incs = {}
        self.is_dma = {}
        self.waited = {e: {} for e in self.eng}
        self.lastw = {}
        self.readers = {}
        self.fence = {}
        self.ninst = 0

    def sem(self, s):
        if s not in self.sems:
            self.sems[s] = self.es.enter_context(self.nc.semaphore("s_" + s))
        return self.sems[s]

    def set_fence(self):
        self.fence = {s: self.count[s] - 1 for s in ("pe", "act", "dve", "pool") if self.count.get(s, 0) > 0}

    def _value(self, s, i):
        if self.is_dma.get(s):
            return 16 * self.count[s]
        lst = self.incs[s]
        p = bisect.bisect_left(lst, i)
        assert p < len(lst), (s, i)
        return p + 1

    def op(self, eng, fn, *args, reads=(), writes=(), dma=None, inc=True, **kw):
        stream = dma if dma else eng
        self.is_dma[stream] = bool(dma)
        idx = self.count.get(stream, 0)
        deps = {}

        def add(s, i):
            if deps.get(s, -1) < i:
                deps[s] = i

        for r in list(reads) + list(writes):
            if r not in self.lastw and r not in self.readers:
                for s, i in self.fence.items():
                    add(s, i)
        for r in reads:
            t = self.lastw.get(r)
            if t:
                add(*t)
        for w in writes:
            t = self.lastw.get(w)
            if t:
                add(*t)
            for s, i in self.readers.get(w, {}).items():
                add(s, i)
        if eng == "pe" and not dma:
            deps.pop("pe", None)
        e = self.eng[eng]
        for s, i in deps.items():
            v = self._value(s, i)
            if self.waited[eng].get(s, 0) < v:
                e.wait_ge(self.sem(s), v)
                self.waited[eng][s] = v
                self.ninst += 1
        ins = fn(*args, **kw)
        self.ninst += 1
        self.count[stream] = idx + 1
        if dma:
            ins.then_inc(self.sem(stream), 16)
        elif inc:
            ins.then_inc(self.sem(stream), 1)
            self.incs.setdefault(stream, []).append(idx)
        tok = (stream, idx)
        for w in writes:
            self.lastw[w] = tok
            self.readers[w] = {}
        for r in reads:
            d = self.readers.setdefault(r, {})
            if d.get(stream, -1) < idx:
                d[stream] = idx
        return ins

    def final_wait(self, eng, streams):
        e = self.eng[eng]
        for s in streams:
            if self.count.get(s, 0) > 0:
                e.wait_ge(self.sem(s), 16 * self.count[s])


def _stream_plan(layers):
    plan = []
    for i in layers:
        for k in range(KD):
            for hf in range(2):
                plan.append(("ada", i, (k, hf)))
        if i == 0:
            for nt in range(8):
                plan.append(("gm_v", i, nt))
            for j in range(KE):
                plan.append(("gm_uz", i, j))
        elif i == 1:
            for j in range(KE):
                plan.append(("cv_ag", i, j))
            for half in range(2):
                for jp in range(KE // 2):
                    plan.append(("cv_z", i, jp))
        elif i == 2:
            for j in range(KE):
                plan.append(("sc_cv", i, j))
                plan.append(("sc_bz", i, j))
        else:
            for gi in range(4):
                plan.append(("pl_p", i, 2 * gi))
                plan.append(("pl_p", i, 2 * gi + 1))
                plan.append(("pl_w", i, gi))
                plan.append(("pl_z", i, 2 * gi))
                plan.append(("pl_z", i, 2 * gi + 1))
        for m in range(KD):
            plan.append(("wo", i, m))
    return plan


def _storage(plan):
    store = []
    seen = {}
    n_ada = 0
    for ent in plan:
        if ent[0] == "ada":
            store.append(("ada", n_ada))
            n_ada += 1
        else:
            if ent not in seen:
                seen[ent] = len(seen)
            store.append(("ws", seen[ent]))
    return store, len(seen), n_ada


def _blk(W, c0):
    b = W[:, c0:c0 + 128].reshape(KD, 128, 128).transpose(1, 0, 2)
    return b.reshape(128, KD * 128)


def _pack_weights(inp, layers):
    plan = _stream_plan(layers)
    store, n_ws, n_ada = _storage(plan)
    ws = np.zeros((n_ws, 128, SLOT), np.float32)
    adaw = np.zeros((n_ada, 128, 1536), np.float32)
    gm_w_in = inp["gm_w_in"][0]
    cv_w_in = inp["cv_w_in"][0]
    sc_w_in = inp["sc_w_in"][0]
    pl_w_in = inp["pl_w_in"][0]
    wout = [inp["gm_w_out"][0], inp["cv_w_out"][0], inp["sc_w_out"][0], inp["pl_w_out"][0]]
    for pi, (kind, i, a) in enumerate(plan):
        si = store[pi][1]
        if kind == "ada":
            k, hf = a
            adaw[si] = inp["ada_w"][i][k * 128:(k + 1) * 128, hf * 1536:(hf + 1) * 1536]
        elif kind == "gm_v":
            c0 = E + a * 256
            ws[si] = gm_w_in[:, c0:c0 + 256].reshape(KD, 128, 256).transpose(1, 0, 2).reshape(128, SLOT)
        elif kind == "gm_uz":
            ws[si, :, 0:1024] = _blk(gm_w_in, a * 128)
            ws[si, :, 1024:2048] = _blk(gm_w_in, 2 * E + a * 128)
        elif kind == "cv_ag":
            ws[si, :, 0:1024] = _blk(cv_w_in, a * 128)
            ws[si, :, 1024:2048] = _blk(cv_w_in, E + a * 128)
        elif kind == "cv_z":
            ws[si, :, 0:1024] = _blk(cv_w_in, 2 * E + (2 * a) * 128)
            ws[si, :, 1024:2048] = _blk(cv_w_in, 2 * E + (2 * a + 1) * 128)
        elif kind == "sc_cv":
            ws[si, :, 0:1024] = _blk(sc_w_in, 1 * E + a * 128)
            ws[si, :, 1024:2048] = _blk(sc_w_in, 2 * E + a * 128)
        elif kind == "sc_bz":
            ws[si, :, 0:1024] = _blk(sc_w_in, 0 * E + a * 128)
            ws[si, :, 1024:2048] = _blk(sc_w_in, 3 * E + a * 128)
        elif kind == "pl_p":
            ws[si, :, 0:1024] = _blk(pl_w_in, (2 * a) * 128)
            ws[si, :, 1024:2048] = _blk(pl_w_in, (2 * a + 1) * 128)
        elif kind == "pl_z":
            ws[si, :, 0:1024] = _blk(pl_w_in, E + (2 * a) * 128)
            ws[si, :, 1024:2048] = _blk(pl_w_in, E + (2 * a + 1) * 128)
        elif kind == "pl_w":
            pw = inp["pl_w"][0][a]
            ws[si] = pw.reshape(4, 128, 512).transpose(1, 0, 2).reshape(128, 2048)
        elif kind == "wo":
            W = wout[i]
            ws[si] = W[:, a * 128:(a + 1) * 128].reshape(KE, 128, 128).transpose(1, 0, 2).reshape(128, KE * 128)
    return ws, adaw


def _fm(v, nch):
    return np.ascontiguousarray(v.reshape(nch, 128).T)


def _pack_smalls(inp):
    sm = np.zeros((128, NSMALL), np.float32)

    def put(name, arr):
        o, w = _off[name]
        sm[:, o:o + w] = arr.reshape(arr.shape[0], -1)

    put("normg", np.stack([_fm(inp["norm_g"][i], KD) for i in range(DEPTH)], axis=1))
    put("gm_lng", _fm(inp["gm_ln_g"][0], KE))
    put("gm_lnb", _fm(inp["gm_ln_b"][0], KE))
    put("cv_dwb", _fm(inp["cv_dw_b"][0], KE))
    put("cv_lng", _fm(inp["cv_ln_g"][0], KE))
    put("cv_lnb", _fm(inp["cv_ln_b"][0], KE))
    put("sc_cw", np.ascontiguousarray(inp["sc_conv_w"][0].reshape(3, KE, 128).transpose(2, 1, 0)))
    put("pl_b", _fm(inp["pl_b"][0].reshape(-1), KE))
    put("pl_s", _fm(inp["pl_scale"][0], KE))
    invc = np.zeros((4, 16), np.float32)
    for gi, w in enumerate(POOL_W):
        for c in range(8):
            for base, col in ((0, c), (S - 8, 8 + c)):
                t = base + c
                lo = max(t - w // 2, 0)
                hi = min(t + w - 1 - w // 2, S - 1)
                invc[gi, col] = 1.0 / (hi - lo + 1)
    put("invc", np.broadcast_to(invc.reshape(1, 64), (128, 64)))
    return sm


def build(layers=(0, 1, 2, 3), nseq=2):
    nc = bass.Bass("TRN2", target_bir_lowering=False)
    plan = _stream_plan(layers)
    store, nslots, n_ada = _storage(plan)
    x_d = nc.dram_tensor("x", [nseq, S, D], F32, kind="ExternalInput").ap()
    cT_d = nc.dram_tensor("cT", [128, KD * 2], F32, kind="ExternalInput").ap()
    sm_d = nc.dram_tensor("smalls", [128, NSMALL], F32, kind="ExternalInput").ap()
    ws_d = nc.dram_tensor("wstream", [nslots, 128, SLOT], F32, kind="ExternalInput").ap()
    ada_d = nc.dram_tensor("adaw", [n_ada, 128, 1536], F32, kind="ExternalInput").ap()
    adab_d = nc.dram_tensor("adab", [2, DEPTH * 3 * D], F32, kind="ExternalInput").ap()
    spw_d = nc.dram_tensor("spwT", [128, 8 * 128], F32, kind="ExternalInput").ap()
    spb_d = nc.dram_tensor("spb", [128, 8 * 128], F32, kind="ExternalInput").ap()
    dww_d = nc.dram_tensor("dww", [128, KE * 31], F32, kind="ExternalInput").ap()
    fing_d = nc.dram_tensor("fing", [128, D], F32, kind="ExternalInput").ap()
    idf_d = nc.dram_tensor("identf", [128, 128], F32, kind="ExternalInput").ap()
    out_d = nc.dram_tensor("out", [nseq, S, D], F32, kind="ExternalOutput").ap()

    with ExitStack() as es:
        P = Prog(nc, es)
        sb = lambda name, shape, dt: es.enter_context(nc.sbuf_tensor(name, shape, dt))
        xT = sb("xT", [128, KD, S], F32)
        hT = sb("hT", [128, KD, S], BF16)
        G = sb("G", [128, KE * S], BF16)
        wbuf = [sb(f"wbuf{i}", [128, SLOT], BF16) for i in range(NBUF)]
        smalls = sb("smalls_sb", [128, NSMALL], F32)
        identf = sb("identf_sb", [128, 128], F32)
        identb = sb("identb", [128, 128], BF16)
        onesb = sb("onesb", [128, 128], BF16)
        cT = sb("cT_sb", [128, KD * 2], F32)
        cact = sb("cact", [128, KD * 2], BF16)
        modT = sb("modT", [128, DEPTH, 24, 2], F32)
        AB = sb("AB", [128, DEPTH, 2, 2, KD], F32)
        pb = [es.enter_context(nc.psum_tensor(f"pb{i}", [128, TW], F32)) for i in range(8)]

        def PB(i):
            return [f"pb{i}_{q}" for q in range(4)]

        def sm(name, c0=0, c1=None):
            o, w = _off[name]
            if c1 is None:
                c1 = w
            return smalls[:, o + c0:o + c1]

        P.op("sp", nc.sync.dma_start, out=smalls[:], in_=sm_d, writes=["smalls"], dma="d_const")
        P.op("sp", nc.sync.dma_start, out=identf[:], in_=idf_d, writes=["identf"], dma="d_const")
        P.op("sp", nc.sync.dma_start, out=cT[:], in_=cT_d, writes=["cT"], dma="d_const")
        P.op("pool", nc.gpsimd.memset, onesb[:], 1.0, writes=["onesb"])
        epsT = sb("epsT", [128, 1], F32)
        P.op("pool", nc.gpsimd.memset, epsT[:], EPS, writes=["epsT"])
        P.op("dve", nc.vector.tensor_copy, out=identb[:], in_=identf[:], reads=["identf"], writes=["identb"])
        P.op("act", nc.scalar.activation, out=cact[:], in_=cT[:], func=AF.Silu, reads=["cT"], writes=["cact"])

        ws_order = []
        for b in range(nseq):
            for si, (kind, i, a) in enumerate(plan):
                if kind == "ada" and b > 0:
                    continue
                ws_order.append((si, kind))
        st = dict(nxt=0, use=0)

        def ws_prefetch():
            n = st["nxt"]
            if n >= len(ws_order):
                return
            st["nxt"] = n + 1
            buf = n % NBUF
            pi, kind = ws_order[n]
            tname, si = store[pi]
            if tname == "ada":
                P.op("pool", nc.gpsimd.dma_start, out=wbuf[buf][:, 0:1536], in_=ada_d[si],
                     writes=[f"wbuf{buf}"], dma=f"d_wb{buf}")
            else:
                P.op("pool", nc.gpsimd.dma_start, out=wbuf[buf][:], in_=ws_d[si],
                     writes=[f"wbuf{buf}"], dma=f"d_wb{buf}")

        def ws_take(kind):
            n = st["use"]
            assert ws_order[n][1] == kind, (n, ws_order[n], kind)
            st["use"] = n + 1
            return n % NBUF

        for _ in range(NBUF):
            ws_prefetch()

        rr = dict(n=0)

        def next_bank(lo=0, hi=8):
            n = rr["n"]
            rr["n"] = n + 1
            return lo + (n % (hi - lo))

        def hres(k, t):
            return f"hT{k}_{t}"

        def xres(k, t):
            return f"xT{k}_{t}"

        def gres(j, t):
            return [f"G{j}_{n}" for n in range(4 * t, 4 * t + 4)]

        def g_std(j, t):
            return G[:, j * S + t * TW: j * S + (t + 1) * TW]

        G4 = G[:].rearrange("p (n c q) -> p n c q", n=NSUB, c=KE)

        def g_blk(j, t):
            return G4[:, 4 * t:4 * t + 4, j, :]

        def mm_in(bk, wb, blk, t):
            for k in range(KD):
                P.op("pe", nc.tensor.matmul, pb[bk][:], wbuf[wb][:, blk * 1024 + k * 128: blk * 1024 + (k + 1) * 128],
                     hT[:, k, t * TW:(t + 1) * TW], start=(k == 0), stop=(k == KD - 1),
                     reads=[f"wbuf{wb}", hres(k, t)], writes=PB(bk), inc=(k == KD - 1))

        C = dict(P=P, nc=nc, sm=sm, smalls=smalls, hT=hT, G=G, G4=G4, wbuf=wbuf, pb=pb, PB=PB, next_bank=next_bank,
                 ws_take=ws_take, ws_prefetch=ws_prefetch, mm_in=mm_in, hres=hres, gres=gres, g_std=g_std,
                 onesb=onesb, identb=identb, epsT=epsT, spw_d=spw_d, spb_d=spb_d, dww_d=dww_d)

        for b in range(nseq):
            with ExitStack() as ls:
                xin = [ls.enter_context(nc.sbuf_tensor(f"xin{b}_{i}", [128, D], F32)) for i in range(2)]
                for s in range(NSUB):
                    xb = s % 2
                    P.op("sp", nc.sync.dma_start, out=xin[xb][:], in_=x_d[b, s * 128:(s + 1) * 128, :],
                         writes=[f"xin{b}_{xb}"], dma=f"d_xin{xb}")
                    for half in range(2):
                        bk = next_bank()
                        for kk in range(4):
                            k = half * 4 + kk
                            P.op("pe", nc.tensor.transpose, pb[bk][:, kk * 128:(kk + 1) * 128],
                                 xin[xb][:, k * 128:(k + 1) * 128], identf[:],
                                 reads=[f"xin{b}_{xb}", "identf"], writes=[f"pb{bk}_{kk}"], inc=(kk == 3))
                        o_ap = xT[:, half * 4:half * 4 + 4, s * 128:(s + 1) * 128]
                        i_ap = pb[bk][:].rearrange("p (k q) -> p k q", k=4)
                        wr = [xres(k2, s // 4) for k2 in range(half * 4, half * 4 + 4)]
                        if half == 0:
                            P.op("act", nc.scalar.activation, out=o_ap, in_=i_ap, func=AF.Copy, reads=PB(bk), writes=wr)
                        else:
                            P.op("dve", nc.vector.tensor_copy, out=o_ap, in_=i_ap, reads=PB(bk), writes=wr)
            P.set_fence()

            for li, i in enumerate(layers):
                pre = f"b{b}L{i}."
                if b == 0:
                    with ExitStack() as ls:
                        mrow = ls.enter_context(nc.sbuf_tensor(pre + "mrow", [2, TW], F32))
                        abt = ls.enter_context(nc.sbuf_tensor(pre + "abt", [2, 3 * D], F32))
                        P.op("sp", nc.sync.dma_start, out=abt[:], in_=adab_d[:, i * 3 * D:(i + 1) * 3 * D], writes=[pre + "abt"], dma="d_const")
                        for k in range(KD):
                            for hf in range(2):
                                wb = ws_take("ada")
                                for c3 in range(3):
                                    ct = hf * 3 + c3
                                    P.op("pe", nc.tensor.matmul, pb[ct][0:2, :], cact[:, 2 * k:2 * k + 2],
                                         wbuf[wb][:, c3 * TW:(c3 + 1) * TW], start=(k == 0), stop=(k == KD - 1),
                                         reads=["cact", f"wbuf{wb}"], writes=PB(ct), inc=(k == KD - 1 or c3 == 2))
                                ws_prefetch()
                        for ct in range(6):
                            P.op("dve", nc.vector.tensor_tensor, out=mrow[:], in0=pb[ct][0:2, :],
                                 in1=abt[:, ct * TW:(ct + 1) * TW], op=ALU.add,
                                 reads=PB(ct) + [pre + "abt"], writes=[pre + "mrow"])
                            for q in range(4):
                                c = ct * 4 + q
                                P.op("pe", nc.tensor.transpose, pb[6][:, c * 2:c * 2 + 2],
                                     mrow[0:2, q * 128:(q + 1) * 128], identf[0:2, 0:2],
                                     reads=[pre + "mrow", "identf"], writes=PB(6), inc=(q == 3))
                        P.op("act", nc.scalar.activation, out=modT[:, i, :, :],
                             in_=pb[6][:, 0:48].rearrange("p (c b) -> p c b", b=2), func=AF.Copy,
                             reads=PB(6), writes=["modT"])
                        for bb in range(2):
                            P.op("dve", nc.vector.tensor_scalar, out=AB[:, i, bb, 0, :], in0=modT[:, i, 8:16, bb],
                                 scalar1=1.0, scalar2=1.0, op0=ALU.add, op1=ALU.mult,
                                 reads=["modT"], writes=["AB"])
                            P.op("dve", nc.vector.tensor_tensor, out=AB[:, i, bb, 0, :], in0=AB[:, i, bb, 0, :],
                                 in1=sm("normg", i * KD, (i + 1) * KD), op=ALU.mult,
                                 reads=["AB", "smalls"], writes=["AB"])
                            P.op("dve", nc.vector.tensor_copy, out=AB[:, i, bb, 1, :], in_=modT[:, i, 0:8, bb],
                                 reads=["modT"], writes=["AB"])
                    P.set_fence()

                with ExitStack() as ls:
                    lsb = lambda name, shape, dt: ls.enter_context(nc.sbuf_tensor(pre + name, shape, dt))
                    sq = [lsb(f"sq{q}", [128, KD, TW], BF16) for q in range(2)]
                    rs = [lsb(f"rs{q}", [128, TW], F32) for q in range(2)]
                    tn = [lsb(f"tn{q}", [128, TW], F32) for q in range(2)]
                    for t in range(NT):
                        q = t % 2
                        P.op("act", nc.scalar.activation, out=sq[q][:], in_=xT[:, :, t * TW:(t + 1) * TW], func=AF.Square,
                             reads=[xres(k, t) for k in range(KD)], writes=[pre + f"sq{q}"])
                        bk = next_bank()
                        for k in range(KD):
                            P.op("pe", nc.tensor.matmul, pb[bk][:], onesb[:], sq[q][:, k, :], start=(k == 0), stop=(k == KD - 1),
                                 reads=["onesb", pre + f"sq{q}"], writes=PB(bk), inc=(k == KD - 1))
                        P.op("act", nc.scalar.activation, out=rs[q][:], in_=pb[bk][:], func=AF.Sqrt, scale=1.0 / D, bias=epsT[:, 0:1],
                             reads=PB(bk) + ["epsT"], writes=[pre + f"rs{q}"])
                        P.op("dve", nc.vector.reciprocal, out=rs[q][:], in_=rs[q][:], reads=[pre + f"rs{q}"], writes=[pre + f"rs{q}"])
                        for k in range(KD):
                            q2 = k % 2
                            P.op("dve", nc.vector.tensor_tensor, out=tn[q2][:], in0=xT[:, k, t * TW:(t + 1) * TW], in1=rs[q][:],
                                 op=ALU.mult, reads=[xres(k, t), pre + f"rs{q}"], writes=[pre + f"tn{q2}"])
                            P.op("act", nc.scalar.activation, out=hT[:, k, t * TW:(t + 1) * TW], in_=tn[q2][:], func=AF.Identity,
                                 bias=AB[:, i, b, 1, k:k + 1], scale=AB[:, i, b, 0, k:k + 1],
                                 reads=[pre + f"tn{q2}", "AB"], writes=[hres(k, t)])
                P.set_fence()

                with ExitStack() as ls:
                    lsb = lambda name, shape, dt: ls.enter_context(nc.sbuf_tensor(pre + name, shape, dt))
                    gt = g_std
                    if i == 0:
                        gt = g_blk
                        _layer_gmlp(C, lsb, pre)
                    elif i == 1:
                        _layer_conf(C, lsb, pre)
                    elif i == 2:
                        _layer_sconv(C, lsb, pre)
                    else:
                        _layer_pool(C, lsb, pre)

                    for m in range(KD):
                        wb = ws_take("wo")
                        for t in range(NT):
                            bk = next_bank()
                            for k in range(KE):
                                P.op("pe", nc.tensor.matmul, pb[bk][:], wbuf[wb][:, k * 128:(k + 1) * 128], gt(k, t),
                                     start=(k == 0), stop=(k == KE - 1),
                                     reads=[f"wbuf{wb}"] + gres(k, t), writes=PB(bk), inc=(k == KE - 1))
                            P.op("dve", nc.vector.scalar_tensor_tensor, out=xT[:, m, t * TW:(t + 1) * TW], in0=pb[bk][:],
                                 scalar=modT[:, i, 16 + m, b:b + 1], in1=xT[:, m, t * TW:(t + 1) * TW],
                                 op0=ALU.mult, op1=ALU.add, reads=PB(bk) + ["modT", xres(m, t)], writes=[xres(m, t)])
                        ws_prefetch()
                P.set_fence()

            with ExitStack() as ls:
                ost = [ls.enter_context(nc.sbuf_tensor(f"ost{b}_{q}", [128, D], F32)) for q in range(2)]
                sqf = [ls.enter_context(nc.sbuf_tensor(f"sqf{b}_{q}", [128, KD, TW], BF16)) for q in range(2)]
                fing = ls.enter_context(nc.sbuf_tensor(f"fing{b}", [128, D], F32))
                rstd = ls.enter_context(nc.sbuf_tensor(f"rstd{b}", [128, NSUB], F32))
                P.op("sp", nc.sync.dma_start, out=fing[:], in_=fing_d, writes=[f"fing{b}"], dma="d_const")
                bs = next_bank()
                for t in range(NT):
                    q = t % 2
                    P.op("act", nc.scalar.activation, out=sqf[q][:], in_=xT[:, :, t * TW:(t + 1) * TW], func=AF.Square,
                         reads=[xres(k, t) for k in range(KD)], writes=[f"sqf{b}_{q}"])
                    for s4 in range(4):
                        s = 4 * t + s4
                        for k in range(KD):
                            P.op("pe", nc.tensor.matmul, pb[bs][:, s:s + 1], sqf[q][:, k, s4 * 128:(s4 + 1) * 128], onesb[:, 0:1],
                                 start=(k == 0), stop=(k == KD - 1), reads=[f"sqf{b}_{q}", "onesb"], writes=PB(bs), inc=(k == KD - 1))
                P.op("act", nc.scalar.activation, out=rstd[:], in_=pb[bs][:, 0:NSUB], func=AF.Sqrt, scale=1.0 / D, bias=epsT[:, 0:1],
                     reads=PB(bs) + ["epsT"], writes=[f"rstd{b}"])
                P.op("dve", nc.vector.reciprocal, out=rstd[:], in_=rstd[:], reads=[f"rstd{b}"], writes=[f"rstd{b}"])
                for s in range(NSUB):
                    q = s % 2
                    b0 = next_bank()
                    b1 = next_bank()
                    for k in range(KD):
                        bk = b0 if k < 4 else b1
                        kk = k % 4
                        P.op("pe", nc.tensor.transpose, pb[bk][:, kk * 128:(kk + 1) * 128], xT[:, k, s * 128:(s + 1) * 128], identf[:],
                             reads=[xres(k, s // 4), "identf"], writes=[f"pb{bk}_{kk}"], inc=(kk == 3))
                    for hh, bk in enumerate((b0, b1)):
                        P.op("dve", nc.vector.scalar_tensor_tensor, out=ost[q][:, hh * TW:(hh + 1) * TW], in0=pb[bk][:],
                             scalar=rstd[:, s:s + 1], in1=fing[:, hh * TW:(hh + 1) * TW], op0=ALU.mult, op1=ALU.mult,
                             reads=PB(bk) + [f"rstd{b}", f"fing{b}"], writes=[f"ost{b}_{q}"])
                    P.op("sp", nc.sync.dma_start, out=out_d[b, s * 128:(s + 1) * 128, :], in_=ost[q][:],
                         reads=[f"ost{b}_{q}"], dma=f"d_out{q}")
            P.set_fence()

        P.final_wait("sp", ["d_out0", "d_out1"])
        build.ninst = P.ninst
    return nc


def _layer_sconv(C, lsb, pre):
    P, nc, sm, pb, PB, next_bank = C["P"], C["nc"], C["sm"], C["pb"], C["PB"], C["next_bank"]
    ws_take, ws_prefetch, mm_in, gres, g_std = C["ws_take"], C["ws_prefetch"], C["mm_in"], C["gres"], C["g_std"]
    prod = [lsb(f"prod{q}", [128, S + 2], F32) for q in range(2)]
    cv = [lsb(f"cv{q}", [128, TW], F32) for q in range(2)]
    tA = [lsb(f"tA{q}", [128, TW], F32) for q in range(2)]
    tB = [lsb(f"tB{q}", [128, TW], F32) for q in range(2)]
    for q in range(2):
        P.op("pool", nc.gpsimd.memset, prod[q][:, 0:1], 0.0, writes=[pre + f"prodL{q}"])
        P.op("pool", nc.gpsimd.memset, prod[q][:, S + 1:S + 2], 0.0, writes=[pre + f"prodR{q}"])
    for j in range(KE):
        pq = j % 2
        pr = pre + f"prod{pq}"
        wb = ws_take("sc_cv")
        for t in range(NT):
            q = t % 2
            b_cg = next_bank()
            b_v = next_bank()
            mm_in(b_cg, wb, 0, t)
            mm_in(b_v, wb, 1, t)
            P.op("act", nc.scalar.activation, out=tA[q][:], in_=pb[b_cg][:], func=AF.Copy, reads=PB(b_cg), writes=[pre + f"tA{q}"])
            P.op("dve", nc.vector.tensor_tensor, out=prod[pq][:, 1 + t * TW:1 + (t + 1) * TW], in0=pb[b_v][:], in1=tA[q][:], op=ALU.mult,
                 reads=PB(b_v) + [pre + f"tA{q}"], writes=[pr + f"_{t}"])
        ws_prefetch()
        wb = ws_take("sc_bz")
        o = j * 3
        for t in range(NT):
            q = t % 2
            rd = [pr + f"_{t}", "smalls"]
            if t > 0:
                rd.append(pr + f"_{t - 1}")
            else:
                rd.append(pre + f"prodL{pq}")
            if t < NT - 1:
                rd.append(pr + f"_{t + 1}")
            else:
                rd.append(pre + f"prodR{pq}")
            P.op("dve", nc.vector.tensor_scalar, out=cv[q][:], in0=prod[pq][:, t * TW:(t + 1) * TW], scalar1=sm("sc_cw", o, o + 1), scalar2=None,
                 op0=ALU.mult, reads=rd, writes=[pre + f"cv{q}"])
            for kk in (1, 2):
                P.op("dve", nc.vector.scalar_tensor_tensor, out=cv[q][:], in0=prod[pq][:, kk + t * TW:kk + (t + 1) * TW],
                     scalar=sm("sc_cw", o + kk, o + kk + 1), in1=cv[q][:],
                     op0=ALU.mult, op1=ALU.add, reads=rd + [pre + f"cv{q}"], writes=[pre + f"cv{q}"])
            b_bg = next_bank()
            b_z = next_bank()
            mm_in(b_bg, wb, 0, t)
            mm_in(b_z, wb, 1, t)
            P.op("act", nc.scalar.activation, out=tA[q][:], in_=pb[b_z][:], func=AF.Silu, reads=PB(b_z), writes=[pre + f"tA{q}"])
            P.op("dve", nc.vector.tensor_tensor, out=tB[q][:], in0=pb[b_bg][:], in1=cv[q][:], op=ALU.mult,
                 reads=PB(b_bg) + [pre + f"cv{q}"], writes=[pre + f"tB{q}"])
            P.op("dve", nc.vector.tensor_tensor, out=g_std(j, t), in0=tB[q][:], in1=tA[q][:], op=ALU.mult,
                 reads=[pre + f"tA{q}", pre + f"tB{q}"], writes=gres(j, t))
        ws_prefetch()


def _layer_pool(C, lsb, pre):
    P, nc, sm, pb, PB, next_bank = C["P"], C["nc"], C["sm"], C["pb"], C["PB"], C["next_bank"]
    ws_take, ws_prefetch, mm_in, gres, g_std, wbuf = C["ws_take"], C["ws_prefetch"], C["mm_in"], C["gres"], C["g_std"], C["wbuf"]
    PAD = 8
    HW = TW + 2 * PAD
    Pp = [lsb(f"Pp{q}", [128, S + 2 * PAD], F32) for q in range(2)]
    Sa = lsb("Sa", [128, HW], F32)
    Sb = lsb("Sb", [128, HW], F32)
    ed = lsb("ed", [128, 16], F32)
    sz = [lsb(f"sz{q}", [128, TW], BF16) for q in range(4)]
    tY = [lsb(f"tY{q}", [128, TW], F32) for q in range(2)]
    for q in range(2):
        P.op("pool", nc.gpsimd.memset, Pp[q][:, 0:PAD], 0.0, writes=[pre + f"PpL{q}"])
        P.op("pool", nc.gpsimd.memset, Pp[q][:, PAD + S:], 0.0, writes=[pre + f"PpR{q}"])
    for gi in range(4):
        w = POOL_W[gi]
        for jh in range(2):
            wb = ws_take("pl_p")
            for j2 in range(2):
                j = 4 * gi + 2 * jh + j2
                pq = j % 2
                pr = pre + f"Pp{pq}"
                for t in range(NT):
                    bk = next_bank()
                    mm_in(bk, wb, j2, t)
                    P.op("act", nc.scalar.activation, out=Pp[pq][:, PAD + t * TW:PAD + (t + 1) * TW], in_=pb[bk][:], func=AF.Copy,
                         reads=PB(bk), writes=[pr + f"_{t}"])
                allp = [pr + f"_{t}" for t in range(NT)] + [pre + f"PpL{pq}", pre + f"PpR{pq}"]
                off = PAD - w // 2
                for t in range(NT):
                    base = t * TW
                    src, src_res, sbase = Pp[pq], allp, base
                    width = 1
                    bufs = [(Sa, pre + "Sa"), (Sb, pre + "Sb")]
                    bi = 0
                    L = HW
                    while width < w:
                        dst, dres = bufs[bi]
                        n = L - width
                        P.op("dve", nc.vector.tensor_tensor, out=dst[:, 0:n], in0=src[:, sbase:sbase + n], in1=src[:, sbase + width:sbase + width + n],
                             op=ALU.add, reads=src_res, writes=[dres])
                        src, src_res, sbase = dst, [dres], 0
                        L = n
                        width *= 2
                        bi ^= 1
                    P.op("dve", nc.vector.scalar_tensor_tensor, out=g_std(j, t), in0=src[:, off:off + TW], scalar=1.0 / w,
                         in1=Pp[pq][:, PAD + base:PAD + base + TW], op0=ALU.mult, op1=ALU.subtract,
                         reads=src_res + allp, writes=gres(j, t))
                    for e0, c0, tt in ((0, 0, 0), (TW - 8, 8, NT - 1)):
                        if t != tt:
                            continue
                        P.op("dve", nc.vector.tensor_tensor, out=ed[:, c0:c0 + 8], in0=src[:, off + e0:off + e0 + 8],
                             in1=sm("invc", gi * 16 + c0, gi * 16 + c0 + 8), op=ALU.mult,
                             reads=src_res + ["smalls"], writes=[pre + f"ed{c0}"])
                        gcol = j * S + t * TW + e0
                        P.op("dve", nc.vector.tensor_tensor, out=C["G"][:, gcol:gcol + 8], in0=ed[:, c0:c0 + 8],
                             in1=Pp[pq][:, PAD + base + e0:PAD + base + e0 + 8],
                             op=ALU.subtract, reads=[pre + f"ed{c0}"] + allp + gres(j, t), writes=gres(j, t))
            ws_prefetch()
        wbp = ws_take("pl_w")
        wbz = [ws_take("pl_z"), ws_take("pl_z")]
        for t in range(NT):
            for jo in range(4):
                bk = next_bank()
                mm_in(bk, wbz[jo // 2], jo % 2, t)
                P.op("act", nc.scalar.activation, out=sz[jo][:], in_=pb[bk][:], func=AF.Silu, reads=PB(bk), writes=[pre + f"sz{jo}"])
            bks = []
            for jo in range(4):
                bk = next_bank()
                bks.append(bk)
                for ji in range(4):
                    P.op("pe", nc.tensor.matmul, pb[bk][:], wbuf[wbp][:, ji * 512 + jo * 128: ji * 512 + (jo + 1) * 128],
                         g_std(4 * gi + ji, t), start=(ji == 0), stop=(ji == 3),
                         reads=[f"wbuf{wbp}"] + gres(4 * gi + ji, t), writes=PB(bk), inc=(ji == 3))
            for jo in range(4):
                j = 4 * gi + jo
                q = jo % 2
                P.op("dve", nc.vector.tensor_scalar, out=tY[q][:], in0=pb[bks[jo]][:], scalar1=sm("pl_b", j, j + 1), scalar2=sm("pl_s", j, j + 1),
                     op0=ALU.add, op1=ALU.mult, reads=PB(bks[jo]) + ["smalls"], writes=[pre + f"tY{q}"])
                P.op("dve", nc.vector.tensor_tensor, out=g_std(j, t), in0=tY[q][:], in1=sz[jo][:], op=ALU.mult,
                     reads=[pre + f"tY{q}", pre + f"sz{jo}"], writes=gres(j, t))
        ws_prefetch()
        ws_prefetch()
        ws_prefetch()


def _layer_gmlp(C, lsb, pre):
    P, nc, sm, pb, PB, next_bank = C["P"], C["nc"], C["sm"], C["pb"], C["PB"], C["next_bank"]
    ws_take, ws_prefetch, mm_in, hres, wbuf = C["ws_take"], C["ws_prefetch"], C["mm_in"], C["hres"], C["wbuf"]
    G4, hT, onesb = C["G4"], C["hT"], C["onesb"]
    spwT = lsb("spwT", [128, 8 * 128], BF16)
    spb = lsb("spb", [128, 8 * 128], F32)
    T2 = lsb("T2", [128, KE * 128], F32)
    stats = lsb("stats", [128, NSUB, 8, 6], F32)
    mv = lsb("mv", [128, NSUB, 2], F32)
    rstd = lsb("rstdv", [128, NSUB], F32)
    tU = [lsb(f"tU{q}", [128, TW], F32) for q in range(4)]
    tZ = [lsb(f"tZ{q}", [128, TW], F32) for q in range(2)]
    P.op("pool", nc.gpsimd.dma_start, out=spwT[:], in_=C["spw_d"], writes=[pre + "spwT"], dma="d_spw")
    P.op("sp", nc.sync.dma_start, out=spb[:], in_=C["spb_d"], writes=[pre + "spb"], dma="d_const")
    for hh in range(2):
        bk = next_bank()
        P.op("pe", nc.tensor.matmul, pb[bk][:], onesb[:], spwT[:, hh * 512:(hh + 1) * 512], start=True, stop=True,
             reads=["onesb", pre + "spwT"], writes=PB(bk))
        for c4 in range(8):
            cj = hh * 8 + c4
            h = cj // 2
            hl = h - hh * 4
            P.op("dve", nc.vector.scalar_tensor_tensor, out=T2[:, cj * 128:(cj + 1) * 128], in0=pb[bk][:, hl * 128:(hl + 1) * 128],
                 scalar=sm("gm_lnb", cj, cj + 1), in1=spb[:, h * 128:(h + 1) * 128], op0=ALU.mult, op1=ALU.add,
                 reads=PB(bk) + [pre + "spb", "smalls"], writes=[pre + f"T2_{cj}"])

    def gr(cj, n):
        return f"G{cj}_{n}"

    for nt in range(8):
        wb = ws_take("gm_v")
        for n in range(NSUB):
            bk = next_bank()
            hq = n % 2
            for k in range(KD):
                P.op("pe", nc.tensor.matmul, pb[bk][:, hq * 256:(hq + 1) * 256], hT[:, k, n * 128:(n + 1) * 128], wbuf[wb][:, k * 256:(k + 1) * 256],
                     start=(k == 0), stop=(k == KD - 1), reads=[f"wbuf{wb}", hres(k, n // 4)],
                     writes=[f"pb{bk}_{2 * hq}", f"pb{bk}_{2 * hq + 1}"], inc=(k == KD - 1))
            wr = [gr(cj, n) for cj in range(2 * nt, 2 * nt + 2)]
            P.op("act", nc.scalar.activation, out=G4[:, n, 2 * nt:2 * nt + 2, :],
                 in_=pb[bk][:, hq * 256:(hq + 1) * 256].rearrange("p (c q) -> p c q", c=2),
                 func=AF.Gelu_apprx_tanh, reads=[f"pb{bk}_{2 * hq}", f"pb{bk}_{2 * hq + 1}"], writes=wr)
            P.op("dve", nc.vector.bn_stats, out=stats[:, n, nt, :], in_=G4[:, n, 2 * nt:2 * nt + 2, :].rearrange("p c q -> p (c q)"),
                 reads=wr, writes=[pre + f"st{n}_{nt}"])
        ws_prefetch()
    for n in range(NSUB):
        P.op("dve", nc.vector.bn_aggr, out=mv[:, n, :], in_=stats[:, n, :, :].rearrange("p a b -> p (a b)"),
             reads=[pre + f"st{n}_{nt}" for nt in range(8)], writes=[pre + f"mv{n}"])
    P.op("act", nc.scalar.activation, out=rstd[:], in_=mv[:, :, 1], func=AF.Sqrt, scale=1.0, bias=C["epsT"][:, 0:1],
         reads=[pre + f"mv{n}" for n in range(NSUB)] + ["epsT"], writes=[pre + "rstd"])
    P.op("dve", nc.vector.reciprocal, out=rstd[:], in_=rstd[:], reads=[pre + "rstd"], writes=[pre + "rstd"])
    for n in range(NSUB):
        allg = [gr(cj, n) for cj in range(KE)]
        P.op("dve", nc.vector.tensor_scalar, out=G4[:, n, :, :], in0=G4[:, n, :, :], scalar1=mv[:, n, 0:1], scalar2=rstd[:, n:n + 1],
             op0=ALU.subtract, op1=ALU.mult, reads=allg + [pre + f"mv{n}", pre + "rstd"], writes=allg)
        for cq in range(4):
            bk = next_bank()
            for c4 in range(4):
                cj = cq * 4 + c4
                h = cj // 2
                P.op("pe", nc.tensor.matmul, pb[bk][:, c4 * 128:(c4 + 1) * 128], G4[:, n, cj, :], spwT[:, h * 128:(h + 1) * 128],
                     start=True, stop=True, reads=[gr(cj, n), pre + "spwT"], writes=[f"pb{bk}_{c4}"], inc=(c4 == 3))
            for c4 in range(4):
                cj = cq * 4 + c4
                P.op("dve", nc.vector.scalar_tensor_tensor, out=G4[:, n, cj, :], in0=pb[bk][:, c4 * 128:(c4 + 1) * 128],
                     scalar=sm("gm_lng", cj, cj + 1), in1=T2[:, cj * 128:(cj + 1) * 128], op0=ALU.mult, op1=ALU.add,
                     reads=[f"pb{bk}_{c4}", "smalls", pre + f"T2_{cj}"], writes=[gr(cj, n)])
    for cj in range(KE):
        wb = ws_take("gm_uz")
        for t in range(NT):
            b_u = next_bank()
            mm_in(b_u, wb, 0, t)
            P.op("act", nc.scalar.activation, out=tU[t][:], in_=pb[b_u][:], func=AF.Gelu_apprx_tanh, reads=PB(b_u), writes=[pre + f"tU{t}"])
        for t in range(NT):
            q = t % 2
            b_z = next_bank()
            mm_in(b_z, wb, 1, t)
            P.op("act", nc.scalar.activation, out=tZ[q][:], in_=pb[b_z][:], func=AF.Silu, reads=PB(b_z), writes=[pre + f"tZ{q}"])
            P.op("dve", nc.vector.tensor_tensor, out=tZ[q][:], in0=tZ[q][:], in1=tU[t][:], op=ALU.mult,
                 reads=[pre + f"tZ{q}", pre + f"tU{t}"], writes=[pre + f"tZ{q}"])
            gv = G4[:, 4 * t:4 * t + 4, cj, :]
            P.op("dve", nc.vector.tensor_tensor, out=gv, in0=gv, in1=tZ[q][:].rearrange("p (n q) -> p n q", n=4), op=ALU.mult,
                 reads=[pre + f"tZ{q}"] + [gr(cj, n) for n in range(4 * t, 4 * t + 4)], writes=[gr(cj, n) for n in range(4 * t, 4 * t + 4)])
        ws_prefetch()


def _layer_conf(C, lsb, pre):
    P, nc, sm, pb, PB, next_bank = C["P"], C["nc"], C["sm"], C["pb"], C["PB"], C["next_bank"]
    ws_take, ws_prefetch, mm_in, gres, g_std = C["ws_take"], C["ws_prefetch"], C["mm_in"], C["gres"], C["g_std"]
    onesb, identb = C["onesb"], C["identb"]
    PADC = 15
    Y = [lsb(f"Y{q}", [128, S + 2 * PADC], BF16) for q in range(1)]
    Dg = lsb("Dg", [128, 31, 128], BF16)
    dww = lsb("dww", [128, KE * 31], F32)
    sg = [lsb(f"sg{q}", [128, TW], F32) for q in range(2)]
    sqc = [lsb(f"sqc{q}", [128, TW], BF16) for q in range(2)]
    t1 = [lsb(f"t1{q}", [128, TW], F32) for q in range(2)]
    rbc = [lsb(f"rbc{q}", [128, TW], F32) for q in range(2)]
    nbc = [lsb(f"nbc{q}", [128, TW], F32) for q in range(2)]
    P.op("sp", nc.sync.dma_start, out=dww[:], in_=C["dww_d"], writes=[pre + "dww"], dma="d_const")
    P.op("pool", nc.gpsimd.memset, Y[0][:, 0:PADC], 0.0, writes=[pre + "YL"])
    P.op("pool", nc.gpsimd.memset, Y[0][:, PADC + S:], 0.0, writes=[pre + "YR"])
    for j in range(KE):
        wb = ws_take("cv_ag")
        yr = pre + "Y"
        for t in range(NT):
            q = t % 2
            b_a = next_bank(0, 6)
            b_g = next_bank(0, 6)
            mm_in(b_a, wb, 0, t)
            mm_in(b_g, wb, 1, t)
            P.op("act", nc.scalar.activation, out=sg[q][:], in_=pb[b_g][:], func=AF.Sigmoid, reads=PB(b_g), writes=[pre + f"sg{q}"])
            P.op("dve", nc.vector.tensor_tensor, out=Y[0][:, PADC + t * TW:PADC + (t + 1) * TW], in0=pb[b_a][:], in1=sg[q][:], op=ALU.mult,
                 reads=PB(b_a) + [pre + f"sg{q}"], writes=[yr + f"_{t}"])
        ws_prefetch()
        P.op("pool", nc.gpsimd.tensor_tensor, out=Dg[:], in0=identb[:].unsqueeze(1).to_broadcast([128, 31, 128]),
             in1=dww[:, j * 31:(j + 1) * 31].unsqueeze(2).to_broadcast([128, 31, 128]), op=ALU.mult,
             reads=["identb", pre + "dww"], writes=[pre + "Dg"])
        for t in range(NT):
            rd = [pre + "Dg", yr + f"_{t}"]
            rd.append(yr + f"_{t - 1}" if t > 0 else pre + "YL")
            rd.append(yr + f"_{t + 1}" if t < NT - 1 else pre + "YR")
            bk = next_bank(0, 6)
            for kk in range(31):
                P.op("pe", nc.tensor.matmul, pb[bk][:], Dg[:, kk, :], Y[0][:, t * TW + kk:t * TW + kk + TW], start=(kk == 0), stop=(kk == 30),
                     reads=rd, writes=PB(bk), inc=(kk == 30))
            P.op("act", nc.scalar.activation, out=g_std(j, t), in_=pb[bk][:], func=AF.Identity, bias=sm("cv_dwb", j, j + 1), scale=1.0,
                 reads=PB(bk) + ["smalls"], writes=gres(j, t))
    for half in range(2):
        tiles = (2 * half, 2 * half + 1)
        for ti, t in enumerate(tiles):
            b1, b2 = 4 + 2 * ti, 5 + 2 * ti
            for j in range(KE):
                q = j % 2
                P.op("act", nc.scalar.activation, out=sqc[q][:], in_=g_std(j, t), func=AF.Square, reads=gres(j, t), writes=[pre + f"sqc{q}"])
                P.op("pe", nc.tensor.matmul, pb[b1][:], onesb[:], g_std(j, t), start=(j == 0), stop=(j == KE - 1),
                     reads=["onesb"] + gres(j, t), writes=PB(b1), inc=(j == KE - 1))
                P.op("pe", nc.tensor.matmul, pb[b2][:], onesb[:], sqc[q][:], start=(j == 0), stop=(j == KE - 1),
                     reads=["onesb", pre + f"sqc{q}"], writes=PB(b2), inc=True)
            P.op("dve", nc.vector.tensor_scalar, out=nbc[ti][:], in0=pb[b1][:], scalar1=1.0 / E, scalar2=None, op0=ALU.mult,
                 reads=PB(b1), writes=[pre + f"nbc{ti}"])
            P.op("dve", nc.vector.tensor_tensor, out=t1[0][:], in0=nbc[ti][:], in1=nbc[ti][:], op=ALU.mult,
                 reads=[pre + f"nbc{ti}"], writes=[pre + "t10"])
            P.op("dve", nc.vector.scalar_tensor_tensor, out=rbc[ti][:], in0=pb[b2][:], scalar=1.0 / E, in1=t1[0][:], op0=ALU.mult, op1=ALU.subtract,
                 reads=PB(b2) + [pre + "t10"], writes=[pre + f"rbc{ti}"])
            P.op("act", nc.scalar.activation, out=rbc[ti][:], in_=rbc[ti][:], func=AF.Sqrt, scale=1.0, bias=C["epsT"][:, 0:1],
                 reads=[pre + f"rbc{ti}", "epsT"], writes=[pre + f"rbc{ti}"])
            P.op("dve", nc.vector.reciprocal, out=rbc[ti][:], in_=rbc[ti][:], reads=[pre + f"rbc{ti}"], writes=[pre + f"rbc{ti}"])
            P.op("dve", nc.vector.scalar_tensor_tensor, out=nbc[ti][:], in0=nbc[ti][:], scalar=-1.0, in1=rbc[ti][:], op0=ALU.mult, op1=ALU.mult,
                 reads=[pre + f"nbc{ti}", pre + f"rbc{ti}"], writes=[pre + f"nbc{ti}"])
        for jp in range(KE // 2):
            wb = ws_take("cv_z")
            for j2 in range(2):
                j = 2 * jp + j2
                for ti, t in enumerate(tiles):
                    q = ti
                    b_z = next_bank(0, 4)
                    mm_in(b_z, wb, j2, t)
                    P.op("act", nc.scalar.activation, out=sg[q][:], in_=pb[b_z][:], func=AF.Silu, reads=PB(b_z), writes=[pre + f"sg{q}"])
                    P.op("dve", nc.vector.tensor_tensor, out=t1[q][:], in0=g_std(j, t), in1=rbc[ti][:], op=ALU.mult,
                         reads=gres(j, t) + [pre + f"rbc{ti}"], writes=[pre + f"t1{q}"])
                    P.op("dve", nc.vector.tensor_tensor, out=t1[q][:], in0=t1[q][:], in1=nbc[ti][:], op=ALU.add,
                         reads=[pre + f"t1{q}", pre + f"nbc{ti}"], writes=[pre + f"t1{q}"])
                    P.op("act", nc.scalar.activation, out=t1[q][:], in_=t1[q][:], func=AF.Silu, bias=sm("cv_lnb", j, j + 1), scale=sm("cv_lng", j, j + 1),
                         reads=[pre + f"t1{q}", "smalls"], writes=[pre + f"t1{q}"])
                    P.op("dve", nc.vector.tensor_tensor, out=g_std(j, t), in0=t1[q][:], in1=sg[q][:], op=ALU.mult,
                         reads=[pre + f"t1{q}", pre + f"sg{q}"], writes=gres(j, t))
            ws_prefetch()


def _in_maps(inputs, layers, nseq, ncores):
    inp = {k: np.asarray(v, dtype=np.float32) for k, v in inputs.items()}
    ws, adaw = _pack_weights(inp, layers)
    sm = _pack_smalls(inp)
    adab = np.ascontiguousarray(np.broadcast_to(inp["ada_b"].reshape(1, -1), (2, DEPTH * 3 * D)))
    spwT = np.ascontiguousarray(inp["gm_sp_w"][0].transpose(2, 0, 1).reshape(128, 8 * 128))
    spb = np.ascontiguousarray(np.broadcast_to(inp["gm_sp_b"][0].reshape(1, 8 * 128), (128, 8 * 128)))
    dww = np.ascontiguousarray(inp["cv_dw_w"][0].reshape(31, KE, 128).transpose(2, 1, 0).reshape(128, KE * 31))
    fing = np.ascontiguousarray(np.broadcast_to(inp["final_g"].reshape(1, D), (128, D)))
    identf = np.eye(128, dtype=np.float32)
    maps = []
    for c in range(ncores):
        xs = np.ascontiguousarray(inp["x"][c * nseq:(c + 1) * nseq])
        cs = inp["c"][c * nseq:(c + 1) * nseq]
        cT = np.zeros((128, KD, 2), np.float32)
        cT[:, :, :nseq] = cs.reshape(nseq, KD, 128).transpose(2, 1, 0)
        maps.append(dict(x=xs, cT=cT.reshape(128, KD * 2), smalls=sm, wstream=ws, adaw=adaw, adab=adab, spwT=spwT, spb=spb,
                         dww=dww, fing=fing, identf=identf))
    return maps


LAUNCH_CORES = 4


def kernel(**inputs):
    nseq = 2
    layers = (0, 1, 2, 3)
    maps = _in_maps(inputs, layers, nseq, N_CORES)
    nc = build(layers, nseq)
    outs = []
    for g in range(N_CORES // LAUNCH_CORES):
        res = run_bass_kernel_spmd(nc, maps[g * LAUNCH_CORES:(g + 1) * LAUNCH_CORES], core_ids=list(range(LAUNCH_CORES)))
        outs += [np.asarray(r["out"], dtype=np.float32) for r in res.results]
    return np.concatenate(outs, axis=0)
```
